# Optimizing a Trainium2 kernel written in Bass

```python
import jax, jax.numpy as jnp
from jax import lax
import numpy as np

D_MODEL = 2048
BATCH = 2
SEQ = 16384
DEPTH = 2

GRID_W = 64
HEAD_DIM = 128
NA_HEADS = D_MODEL // HEAD_DIM
NA_WIN_ROWS = 8
NA_WIN_COLS = 16
DIL_GROUPS = ((128, 1), (512, 4), (2048, 16))
DIL_HEADS = 8
ROPE_THETA = 500000.0
ROPE_DIM = HEAD_DIM // 4
D_FF = 4 * D_MODEL
N_MIXERS = 2
BLOCK_Q = 128
EPS = 1e-6
NEG = -1e30

kernel_name = "hybrid_natten_dilated_encoder"


def rms_norm(x, g):
    xf = x.astype(jnp.float32)
    y = xf * lax.rsqrt(jnp.mean(xf * xf, axis=-1, keepdims=True) + EPS)
    return (y * g.astype(jnp.float32)).astype(x.dtype)


def sq_relu_mlp(h, w1, w2):
    return jnp.square(jax.nn.relu(h @ w1)) @ w2


def neighbourhood_attention(h, w_qkv, rpb, w_o):
    B, T, _ = h.shape
    rows = T // GRID_W
    kh = min(NA_WIN_ROWS, rows)
    kw = NA_WIN_COLS
    qkv = (h @ w_qkv).reshape(B, rows, GRID_W, 3, NA_HEADS, HEAD_DIM)
    q = qkv[:, :, :, 0] * (HEAD_DIM ** -0.5)
    k = qkv[:, :, :, 1]
    v = qkv[:, :, :, 2]
    cols = np.arange(GRID_W)
    col_start = np.clip(cols - kw // 2, 0, GRID_W - kw)
    col_mask = (cols[None, :] >= col_start[:, None]) & (cols[None, :] < col_start[:, None] + kw)
    col_idx = np.clip(cols[None, :] - cols[:, None] + NA_WIN_COLS - 1, 0, 2 * NA_WIN_COLS - 2)
    rpb_cols = rpb[:, :, col_idx].transpose(0, 2, 1, 3)
    mask = jnp.asarray(col_mask)[:, None, :]

    def one_row(r):
        rs = jnp.clip(r - kh // 2, 0, rows - kh)
        q_r = lax.dynamic_index_in_dim(q, r, axis=1, keepdims=False)
        k_r = lax.dynamic_slice_in_dim(k, rs, kh, axis=1)
        v_r = lax.dynamic_slice_in_dim(v, rs, kh, axis=1)
        row_off = rs + jnp.arange(kh) - r
        bias = jnp.take(rpb_cols, row_off + NA_WIN_ROWS - 1, axis=2)
        s = jnp.einsum('bqhd,brwhd->bhqrw', q_r, k_r).astype(jnp.float32)
        s = s + bias.astype(jnp.float32)[None]
        s = jnp.where(mask, s, NEG)
        p = jax.nn.softmax(s, axis=(-2, -1))
        return jnp.einsum('bhqrw,brwhd->bqhd', p.astype(v.dtype), v_r)

    out = lax.map(one_row, jnp.arange(rows))
    out = out.transpose(1, 0, 2, 3, 4).reshape(B, T, NA_HEADS * HEAD_DIM)
    return out @ w_o


def apply_partial_rotary(x, cos, sin):
    half = ROPE_DIM // 2
    x1 = x[..., :half]
    x2 = x[..., half:ROPE_DIM]
    return jnp.concatenate([x1 * cos - x2 * sin, x2 * cos + x1 * sin, x[..., ROPE_DIM:]], axis=-1)


def banded_attention(q, k, v, half):
    L, hd = q.shape[-2], q.shape[-1]
    lead = q.shape[:-2]
    nl = len(lead)
    qb = min(BLOCK_Q, L)
    nb = -(-L // qb)
    pad = nb * qb - L
    qp = jnp.pad(q, [(0, 0)] * nl + [(0, pad), (0, 0)]).reshape(*lead, nb, qb, hd)
    kp = jnp.pad(k, [(0, 0)] * nl + [(half, half + pad), (0, 0)])
    vp = jnp.pad(v, [(0, 0)] * nl + [(half, half + pad), (0, 0)])
    idx = np.arange(nb)[:, None] * qb + np.arange(qb + 2 * half)[None, :]
    kb = jnp.take(kp, idx, axis=-2)
    vb = jnp.take(vp, idx, axis=-2)
    qpos = np.arange(nb)[:, None] * qb + np.arange(qb)[None, :]
    kpos = idx - half
    valid = ((np.abs(kpos[:, None, :] - qpos[:, :, None]) <= half)
             & (kpos[:, None, :] >= 0) & (kpos[:, None, :] < L))
    s = jnp.einsum('...nqd,...nkd->...nqk', qp, kb).astype(jnp.float32)
    s = jnp.where(jnp.asarray(valid), s, NEG)
    lse = jax.nn.logsumexp(s, axis=-1)
    p = jnp.exp(s - lse[..., None])
    o = jnp.einsum('...nqk,...nkd->...nqd', p.astype(v.dtype), vb)
    o = o.reshape(*lead, nb * qb, hd)[..., :L, :]
    lse = lse.reshape(*lead, nb * qb)[..., :L]
    return o, lse


def dilated_attention(h, w_qkv, w_o):
    B, T, _ = h.shape
    G = len(DIL_GROUPS)
    qkv = (h @ w_qkv).reshape(B, T, G, 3, DIL_HEADS, HEAD_DIM)
    pos = jnp.arange(T, dtype=jnp.float32)
    inv_freq = ROPE_THETA ** (-jnp.arange(0, ROPE_DIM, 2, dtype=jnp.float32) / ROPE_DIM)
    ang = pos[:, None] * inv_freq[None, :]
    cos = jnp.cos(ang)[:, None, :].astype(h.dtype)
    sin = jnp.sin(ang)[:, None, :].astype(h.dtype)
    outs, lses = [], []
    for g, (window, dil) in enumerate(DIL_GROUPS):
        L = T // dil
        n_side = (window // 2) // dil

        def split(t):
            return t.reshape(B, L, dil, DIL_HEADS, HEAD_DIM).transpose(0, 3, 2, 1, 4)

        q = apply_partial_rotary(qkv[:, :, g, 0], cos, sin) * (HEAD_DIM ** -0.5)
        k = apply_partial_rotary(qkv[:, :, g, 1], cos, sin)
        v = qkv[:, :, g, 2]
        o, lse = banded_attention(split(q), split(k), split(v), n_side)
        outs.append(o.transpose(0, 3, 2, 1, 4).reshape(B, T, DIL_HEADS, HEAD_DIM))
        lses.append(lse.transpose(0, 3, 2, 1).reshape(B, T, DIL_HEADS))
    wts = jax.nn.softmax(jnp.stack(lses, axis=0), axis=0)
    o = jnp.einsum('gbth,gbthd->bthd', wts.astype(h.dtype), jnp.stack(outs, axis=0))
    return o.reshape(B, T, DIL_HEADS * HEAD_DIM) @ w_o


def setup_inputs(seed: int = 0) -> dict:
    key = jax.random.key(seed)
    ks = jax.random.split(key, 16)
    f32 = jnp.float32
    G = len(DIL_GROUPS)

    def w(k, shape, fan_in):
        return jax.random.normal(k, shape, f32) * (fan_in ** -0.5)

    def gain(k):
        return 1.0 + 0.02 * jax.random.normal(k, (D_MODEL,), f32)

    return {
        "x": jax.random.normal(ks[0], (BATCH, SEQ, D_MODEL), f32),
        "na_norm": gain(ks[1]),
        "na_wqkv": w(ks[2], (D_MODEL, 3 * NA_HEADS * HEAD_DIM), D_MODEL),
        "na_rpb": 0.1 * jax.random.normal(ks[3], (NA_HEADS, 2 * NA_WIN_ROWS - 1, 2 * NA_WIN_COLS - 1), f32),
        "na_wo": w(ks[4], (NA_HEADS * HEAD_DIM, D_MODEL), NA_HEADS * HEAD_DIM),
        "ffn0_norm": gain(ks[5]),
        "ffn0_w1": w(ks[6], (D_MODEL, D_FF), D_MODEL),
        "ffn0_w2": w(ks[7], (D_FF, D_MODEL), D_FF),
        "dil_norm": gain(ks[8]),
        "dil_wqkv": w(ks[9], (D_MODEL, G * 3 * DIL_HEADS * HEAD_DIM), D_MODEL),
        "dil_wo": w(ks[10], (DIL_HEADS * HEAD_DIM, D_MODEL), DIL_HEADS * HEAD_DIM),
        "ffn1_norm": gain(ks[11]),
        "ffn1_w1": w(ks[12], (D_MODEL, D_FF), D_MODEL),
        "ffn1_w2": w(ks[13], (D_FF, D_MODEL), D_FF),
        "final_norm": gain(ks[14]),
    }


def reference(x, na_norm, na_wqkv, na_rpb, na_wo, ffn0_norm, ffn0_w1, ffn0_w2,
              dil_norm, dil_wqkv, dil_wo, ffn1_norm, ffn1_w1, ffn1_w2, final_norm):
    mixer_norms = (na_norm, dil_norm)
    ffns = ((ffn0_norm, ffn0_w1, ffn0_w2), (ffn1_norm, ffn1_w1, ffn1_w2))
    for i in range(DEPTH):
        m = i % N_MIXERS
        hn = rms_norm(x, mixer_norms[m])
        if m == 0:
            x = x + neighbourhood_attention(hn, na_wqkv, na_rpb, na_wo)
        else:
            x = x + dilated_attention(hn, dil_wqkv, dil_wo)
        g, w1, w2 = ffns[i]
        x = x + sq_relu_mlp(rms_norm(x, g), w1, w2)
    return rms_norm(x, final_norm)
```

```python
import contextlib
import numpy as np
import ml_dtypes
import concourse.bass as bass
import concourse.mybir as mybir
from concourse.bass_utils import run_bass_kernel_spmd

F32 = mybir.dt.float32
BF16 = mybir.dt.bfloat16
AF = mybir.ActivationFunctionType
ALU = mybir.AluOpType
AX = mybir.AxisListType

ENGS = ("sync", "scalar", "vector", "gpsimd", "tensor")


class Buf:
    __slots__ = ("name", "w", "r")

    def __init__(self, name=""):
        self.name = name
        self.w = None
        self.r = []


class Op:
    __slots__ = ("eng", "fn", "deps", "dma", "sig", "slot", "slotval", "used", "idx")

    def __init__(self, eng, fn, dma):
        self.eng = eng
        self.fn = fn
        self.deps = []
        self.dma = dma
        self.sig = 0
        self.slot = -1
        self.slotval = 0
        self.used = False
        self.idx = 0


class Prog:
    _ntag = [0]

    def __init__(self, nc, strict=True, ndma=8):
        Prog._ntag[0] += 1
        self.tag = "p%d_" % Prog._ntag[0]
        self.nc = nc
        self.strict = strict
        self.ndma = ndma
        self.ops = {e: [] for e in ENGS}
        self.stack = contextlib.ExitStack()
        self.nops = 0

    def sbuf(self, name, shape, dtype):
        return self.stack.enter_context(self.nc.sbuf_tensor(self.tag + name, list(shape), dtype))

    def psum(self, name, shape, dtype):
        return self.stack.enter_context(self.nc.psum_tensor(self.tag + name, list(shape), dtype))

    def add(self, eng, fn, reads=(), writes=(), dma=False):
        o = Op(eng, fn, dma)
        o.idx = self.nops
        self.nops += 1
        deps = {}
        for b in writes:
            if b.w is not None:
                deps[id(b.w)] = (b.w, False)
            for r in b.r:
                deps[id(r)] = (r, False)
        for b in reads:
            if b.w is not None:
                deps[id(b.w)] = (b.w, True)
        for b in reads:
            b.r.append(o)
        for b in writes:
            b.w = o
            b.r = []
        deps.pop(id(o), None)
        o.deps = list(deps.values())
        self.ops[eng].append(o)
        return o

    def dma(self, eng, out, in_, reads=(), writes=(), **kw):
        return self.add(eng, lambda e: e.dma_start(out=out, in_=in_, **kw),
                        reads=reads, writes=writes, dma=True)

    def _needs_wait(self, op, dk):
        d, raw = dk
        if d.dma or op.dma:
            return True
        if d.eng != op.eng:
            return True
        if op.eng == "tensor":
            return False
        return raw and self.strict

    def emit(self):
        nc = self.nc
        for e in ENGS:
            for op in self.ops[e]:
                for dk in op.deps:
                    if self._needs_wait(op, dk):
                        dk[0].used = True
        finals = {e: [] for e in ENGS}
        for e in ENGS:
            for op in self.ops[e]:
                if op.dma and not op.used:
                    op.used = True
                    finals[e].append(op)
        esem = {}
        dsem = {}
        for e in ENGS:
            if not self.ops[e]:
                continue
            esem[e] = self.stack.enter_context(nc.semaphore(self.tag + "s_" + e))
            ndma = sum(1 for o in self.ops[e] if o.dma)
            if ndma:
                dsem[e] = [self.stack.enter_context(nc.semaphore(self.tag + "d_%s%d" % (e, i)))
                           for i in range(min(self.ndma, ndma))]
        for e in ENGS:
            cnt = 0
            di = 0
            for op in self.ops[e]:
                if op.dma:
                    ns = len(dsem[e])
                    op.slot = di % ns
                    op.slotval = 16 * (di // ns + 1)
                    di += 1
                elif op.used:
                    cnt += 1
                    op.sig = cnt

        def event(d):
            if d.dma:
                return dsem[d.eng][d.slot], d.slotval
            return esem[d.eng], d.sig

        def run(e, eng):
            waited = {}

            def wait(sem, val):
                k = id(sem)
                if waited.get(k, 0) >= val:
                    return
                waited[k] = val
                eng.wait_ge(sem, val)

            for op in self.ops[e]:
                for dk in op.deps:
                    if self._needs_wait(op, dk):
                        wait(*event(dk[0]))
                if op.dma:
                    sem = dsem[e][op.slot]
                    if op.slotval > 16:
                        wait(sem, op.slotval - 16)
                    op.fn(eng).then_inc(sem, 16)
                else:
                    ins = op.fn(eng)
                    if op.used:
                        ins.then_inc(esem[e], 1)
            for op in finals[e]:
                wait(*event(op))

        with nc.Block() as block:
            if self.ops["sync"]:
                block.sync(lambda eng: run("sync", eng))
            if self.ops["scalar"]:
                block.scalar(lambda eng: run("scalar", eng))
            if self.ops["vector"]:
                block.vector(lambda eng: run("vector", eng))
            if self.ops["gpsimd"]:
                block.gpsimd(lambda eng: run("gpsimd", eng))
            if self.ops["tensor"]:
                block.tensor(lambda eng: run("tensor", eng))
        self.stack.close()


NCORES = 8
D = 2048
DC = D // 128
DFF = 8192
SEQ = 16384
NOWN = 4096
TB = 512
H0 = 256
E0 = NOWN + 2 * H0
H1 = 1024
E1 = NOWN + 2 * H1
QSCALE = 128 ** -0.5
NEGB = -30000.0


def make_ident(P, name="ident"):
    identf = P.sbuf(name + "f", [128, 128], F32)
    ident = P.sbuf(name, [128, 128], BF16)
    b = Buf(name)
    P.add("gpsimd", lambda e: e.memset(identf[:], 0.0), writes=[b])
    P.add("gpsimd", lambda e: e.affine_select(
        out=identf[:], in_=identf[:], pattern=[[-1, 128]], compare_op=ALU.not_equal,
        fill=1.0, base=0, channel_multiplier=1), reads=[b], writes=[b])
    P.add("gpsimd", lambda e: e.tensor_copy(ident[:], identf[:]), reads=[b], writes=[b])
    return ident, b


def cast_weights(nc, pairs, width=2048):
    P = Prog(nc)
    src_t = SbufRot(P, "cin", 4, [128, width], F32)
    dst_t = SbufRot(P, "cout", 4, [128, width], BF16)
    engs = ("scalar", "vector", "gpsimd")
    n = 0
    for dst, src in pairs:
        K, F = src.shape
        for r0 in range(0, K, 128):
            for c0 in range(0, F, width):
                w = min(width, F - c0)
                a, b_a = src_t.next()
                o, b_o = dst_t.next()
                P.dma("sync", a[:, :w], src[r0:r0 + 128, c0:c0 + w], writes=[b_a])
                evac(P, engs[n % 3], o[:, :w], a[:, :w], [b_a], [b_o])
                P.dma("gpsimd" if n % 2 else "scalar", dst[r0:r0 + 128, c0:c0 + w], o[:, :w], reads=[b_o])
                n += 1
    P.emit()


class WStream:
    def __init__(self, P, name, nbuf=3, kc=16, width=512):
        self.P = P
        self.nbuf = nbuf
        self.tiles = [P.sbuf("%s%d" % (name, i), [128, kc, width], BF16) for i in range(nbuf)]
        self.bufs = [Buf("%s%d" % (name, i)) for i in range(nbuf)]
        self.i = 0

    def load(self, src, kc=None):
        s = self.i % self.nbuf
        self.i += 1
        dst = self.tiles[s][:] if kc is None else self.tiles[s][:, :kc, :]
        self.P.dma("sync", dst, src, writes=[self.bufs[s]])
        return self.tiles[s], self.bufs[s]


def wview(w, r0, nrows, c0, ncols):
    return w[r0:r0 + nrows, c0:c0 + ncols].rearrange("(c p) f -> p c f", p=128)


class Norm:
    def __init__(self, P, name, gains):
        self.P = P
        self.gT = []
        self.b_gT = Buf(name + "gT")
        for i, g in enumerate(gains):
            t = P.sbuf("%sgT%d" % (name, i), [128, DC], F32)
            P.dma("sync", t[:], g.rearrange("(c p) -> p c", p=128), writes=[self.b_gT],
                  allow_slow_non_contiguous=True)
            self.gT.append(t)
        self.junk = P.sbuf(name + "junk", [128, D], BF16)
        self.b_junk = Buf()
        self.eps = P.sbuf(name + "eps", [128, 1], F32)
        self.b_eps = Buf()
        P.add("gpsimd", lambda e: e.memset(self.eps[:], 1e-6), writes=[self.b_eps])
        self.ss = [P.sbuf("%sss%d" % (name, i), [128, 1], F32) for i in range(2)]
        self.rstd = [P.sbuf("%srstd%d" % (name, i), [128, 1], F32) for i in range(2)]
        self.xn = [P.sbuf("%sxn%d" % (name, i), [128, D], BF16) for i in range(2)]
        self.b_ss = [Buf() for _ in range(2)]
        self.b_rstd = [Buf() for _ in range(2)]
        self.b_xn = [Buf() for _ in range(2)]
        self.pt = [P.psum("%spt%d" % (name, i), [128, 8, 128], BF16) for i in range(2)]
        self.b_pt = [Buf() for _ in range(2)]
        self.ident, self.b_ident = make_ident(P, name + "id")
        self.i = 0
        self.ti = 0

    def stats(self, src, b_src):
        P = self.P
        s = self.i % 2
        self.i += 1
        ss, rstd = self.ss[s], self.rstd[s]
        P.add("scalar", lambda e: e.activation(out=self.junk[:], in_=src, func=AF.Square, accum_out=ss[:]),
              reads=[b_src], writes=[self.b_junk, self.b_ss[s]])
        P.add("scalar", lambda e: e.activation(out=rstd[:], in_=ss[:], func=AF.Sqrt, scale=1.0 / D,
                                                bias=self.eps[:]),
              reads=[self.b_ss[s], self.b_eps], writes=[self.b_rstd[s]])
        P.add("vector", lambda e: e.reciprocal(out=rstd[:], in_=rstd[:]),
              reads=[self.b_rstd[s]], writes=[self.b_rstd[s]])
        return s

    def run(self, src, b_src, dstT, b_dst, gi=0):
        P = self.P
        s = self.stats(src, b_src)
        rstd, xn = self.rstd[s], self.xn[s]
        P.add("scalar", lambda e: e.activation(out=xn[:], in_=src, func=AF.Copy, scale=rstd[:]),
              reads=[b_src, self.b_rstd[s]], writes=[self.b_xn[s]])
        gT = self.gT[gi]
        for half in range(DC // 8):
            k = self.ti % 2
            self.ti += 1
            pt = self.pt[k]
            for c in range(8):
                dc = half * 8 + c
                P.add("tensor", lambda e, c=c, dc=dc, pt=pt: e.transpose(
                    out=pt[:, c, :], in_=xn[:, dc * 128:(dc + 1) * 128], identity=self.ident[:]),
                    reads=[self.b_xn[s], self.b_ident], writes=[self.b_pt[k]])
            P.add("vector", lambda e, half=half, pt=pt: e.tensor_tensor(
                out=dstT[:, half * 8:(half + 1) * 8, :], in0=pt[:],
                in1=gT[:, half * 8:(half + 1) * 8].unsqueeze(2).to_broadcast([128, 8, 128]), op=ALU.mult),
                reads=[self.b_pt[k], self.b_gT], writes=[b_dst])


class PsumRot:
    def __init__(self, P, name, n, shape=(128, 512), dtype=F32):
        self.t = [P.psum("%s%d" % (name, i), list(shape), dtype) for i in range(n)]
        self.b = [Buf("%s%d" % (name, i)) for i in range(n)]
        self.i = 0
        self.n = n

    def next(self):
        k = self.i % self.n
        self.i += 1
        return self.t[k], self.b[k]


class SbufRot:
    def __init__(self, P, name, n, shape, dtype):
        self.t = [P.sbuf("%s%d" % (name, i), list(shape), dtype) for i in range(n)]
        self.b = [Buf("%s%d" % (name, i)) for i in range(n)]
        self.i = 0
        self.n = n

    def next(self):
        k = self.i % self.n
        self.i += 1
        return self.t[k], self.b[k]


def load_norm_block(P, nrm, xrot, x_dram, tok0, xnT, b_xnT, gi=0):
    for tt in range(TB // 128):
        xt, bx = xrot.next()
        P.dma("sync", xt[:], x_dram[tok0 + tt * 128: tok0 + (tt + 1) * 128, :], writes=[bx])
        nrm.run(xt[:], bx, xnT[:, :, tt * 128:(tt + 1) * 128], b_xnT, gi)


def mm_ws(P, ps, b_ps, wt, b_wt, fl, xnT, b_xnT, kc=DC, n=TB):
    for c in range(kc):
        P.add("tensor", lambda e, c=c: e.matmul(ps[:, :n], lhsT=wt[:, c, fl * 128:(fl + 1) * 128],
                                                 rhs=xnT[:, c, :n], start=(c == 0), stop=(c == kc - 1)),
              reads=[b_wt] + _bl(b_xnT), writes=[b_ps])


def mm_as(P, ps, b_ps, wt, b_wt, xnT, b_xnT, tt, kc=DC, first=True, last=True, width=512, c0=0):
    for c in range(kc):
        P.add("tensor", lambda e, c=c: e.matmul(ps[:, :width], lhsT=xnT[:, c0 + c, tt * 128:(tt + 1) * 128],
                                                 rhs=wt[:, c, :width], start=(first and c == 0),
                                                 stop=(last and c == kc - 1)),
              reads=[b_wt] + _bl(b_xnT), writes=[b_ps])


def _bl(b):
    return list(b) if isinstance(b, (list, tuple)) else [b]


def evac(P, eng, out, in_, reads, writes, scale=None):
    if eng == "scalar":
        if scale is None:
            P.add("scalar", lambda e: e.copy(out=out, in_=in_), reads=reads, writes=writes)
        else:
            P.add("scalar", lambda e: e.activation(out=out, in_=in_, func=AF.Copy, scale=float(scale)),
                  reads=reads, writes=writes)
    else:
        if scale is None:
            P.add(eng, lambda e: e.tensor_copy(out=out, in_=in_), reads=reads, writes=writes)
        else:
            P.add(eng, lambda e: e.tensor_scalar(out=out, in0=in_, scalar1=float(scale), scalar2=None,
                                                 op0=ALU.mult), reads=reads, writes=writes)


def phase_qkv0(nc, x_ext, g_na, wqkv_b, qT0, kT0, v0, nblk=E0 // TB):
    P = Prog(nc)
    nrm = Norm(P, "n1", [g_na])
    xrot = SbufRot(P, "xin", 2, [128, D], F32)
    xnT = SbufRot(P, "xnT", 2, [128, DC, TB], BF16)
    ws = WStream(P, "w", 3)
    psr = PsumRot(P, "ps", 6)
    stqk = [P.sbuf("stqk%d" % i, [128, 4, TB], BF16) for i in range(2)]
    b_stqk = [[Buf() for _ in range(4)] for _ in range(2)]
    stv = SbufRot(P, "stv", 3, [128, 512], BF16)
    n_ev = 0
    n_st = 0
    for b in range(nblk):
        xT, b_xT = xnT.next()
        load_norm_block(P, nrm, xrot, x_ext, b * TB, xT, b_xT)
        for wi in range(8):
            wt, b_wt = ws.load(wview(wqkv_b, 0, D, wi * 512, 512))
            k = n_st % 2
            n_st += 1
            st, bst = stqk[k], b_stqk[k]
            for fl in range(4):
                ps, b_ps = psr.next()
                mm_ws(P, ps, b_ps, wt, b_wt, fl, xT, b_xT)
                eng = "scalar" if n_ev % 2 == 0 else "vector"
                n_ev += 1
                evac(P, eng, st[:, fl, :], ps[:], [b_ps], [bst[fl]], scale=(QSCALE if wi < 4 else None))
            dst = qT0 if wi < 4 else kT0
            r0 = (wi % 4) * 512
            P.dma("gpsimd", dst[r0:r0 + 512, b * TB:(b + 1) * TB].rearrange("(c p) t -> p c t", p=128),
                  st[:], reads=bst)
        for wi in range(4):
            wt, b_wt = ws.load(wview(wqkv_b, 0, D, 4096 + wi * 512, 512))
            for tt in range(4):
                ps, b_ps = psr.next()
                mm_as(P, ps, b_ps, wt, b_wt, xT, b_xT, tt)
                st, b_st = stv.next()
                eng = "scalar" if n_ev % 2 == 0 else "vector"
                n_ev += 1
                evac(P, eng, st[:], ps[:], [b_ps], [b_st])
                P.dma("gpsimd", v0[b * TB + tt * 128: b * TB + (tt + 1) * 128, wi * 512:(wi + 1) * 512],
                      st[:], reads=[b_st])
    P.emit()


def phase_na(nc, qT0, kT0, v0, bint, bfirst, blast, oT0, nheads=16, nblk=NOWN // TB):
    P = Prog(nc)
    ones = P.sbuf("ones", [128, 128], BF16)
    b_ones = Buf()
    P.add("gpsimd", lambda e: e.memset(ones[:], 1.0), writes=[b_ones])
    kq = SbufRot(P, "kq", 2, [128, 2, E0], BF16)
    vv = SbufRot(P, "vv", 2, [128, E0 // 128, 128], BF16)
    bi = SbufRot(P, "bi", 2, [128, 22 * 64], F32)
    bf = SbufRot(P, "bf", 2, [128, 8, 256], F32)
    bl = SbufRot(P, "bl", 2, [128, 8, 256], F32)
    psS = PsumRot(P, "psS", 3)
    psO = PsumRot(P, "psO", 2)
    psD = PsumRot(P, "psD", 2)
    sb = SbufRot(P, "sb", 2, [128, 512], F32)
    pT = SbufRot(P, "pT", 3, [128, 512], BF16)
    rden = SbufRot(P, "rden", 2, [128, 512], F32)
    ost = SbufRot(P, "ost", 2, [128, 512], BF16)
    res = {}

    def get_head(h):
        if h in res or h >= nheads:
            return res.get(h)
        kqt, b_kq = kq.next()
        P.dma("sync", kqt[:, 0, :], kT0[h * 128:(h + 1) * 128, :], writes=[b_kq])
        P.dma("sync", kqt[:, 1, :], qT0[h * 128:(h + 1) * 128, :], writes=[b_kq])
        vt, b_v = vv.next()
        P.dma("sync", vt[:], v0[:, h * 128:(h + 1) * 128].rearrange("(t p) d -> p t d", p=128), writes=[b_v])
        bit, b_bi = bi.next()
        P.dma("sync", bit[:], bint[h], writes=[b_bi])
        bft, b_bf = bf.next()
        P.dma("sync", bft[:], bfirst[h].rearrange("p (j q) -> p j q", j=8), writes=[b_bf])
        blt, b_bl = bl.next()
        P.dma("sync", blt[:], blast[h].rearrange("p (j q) -> p j q", j=8), writes=[b_bl])
        res[h] = (kqt, b_kq, vt, b_v, bit, b_bi, bft, b_bf, blt, b_bl)
        return res[h]

    steps = [(h, blk, j) for h in range(nheads) for blk in range(nblk) for j in range(8)]

    def issue_S(st):
        h, blk, j = st
        kqt, b_kq = get_head(h)[:2]
        ps, b_ps = psS.next()
        P.add("tensor", lambda e: e.matmul(ps[:], lhsT=kqt[:, 0, TB * blk + 128 * j: TB * blk + 128 * j + 128],
                                           rhs=kqt[:, 1, H0 + TB * blk: H0 + TB * blk + TB], start=True, stop=True),
              reads=[b_kq], writes=[b_ps])
        return ps, b_ps

    nxt = issue_S(steps[0])
    acc = None
    for i, (h, blk, j) in enumerate(steps):
        ps, b_ps = nxt
        if i + 1 < len(steps):
            nxt = issue_S(steps[i + 1])
        if blk == 0 and j == 0:
            get_head(h + 1)
        kqt, b_kq, vt, b_v, bit, b_bi, bft, b_bf, blt, b_bl = get_head(h)
        s_t, b_s = sb.next()
        lo = (14 - 2 * j) * 64
        if blk == 0:
            P.add("vector", lambda e, s_t=s_t, ps=ps, bft=bft, j=j: e.tensor_tensor(
                out=s_t[:, 0:256], in0=ps[:, 0:256], in1=bft[:, j, :], op=ALU.add),
                reads=[b_ps, b_bf], writes=[b_s])
            P.add("vector", lambda e, s_t=s_t, ps=ps, bit=bit, lo=lo: e.tensor_tensor(
                out=s_t[:, 256:512], in0=ps[:, 256:512], in1=bit[:, lo + 256: lo + 512], op=ALU.add),
                reads=[b_ps, b_bi], writes=[b_s])
        elif blk == nblk - 1:
            P.add("vector", lambda e, s_t=s_t, ps=ps, bit=bit, lo=lo: e.tensor_tensor(
                out=s_t[:, 0:256], in0=ps[:, 0:256], in1=bit[:, lo: lo + 256], op=ALU.add),
                reads=[b_ps, b_bi], writes=[b_s])
            P.add("vector", lambda e, s_t=s_t, ps=ps, blt=blt, j=j: e.tensor_tensor(
                out=s_t[:, 256:512], in0=ps[:, 256:512], in1=blt[:, j, :], op=ALU.add),
                reads=[b_ps, b_bl], writes=[b_s])
        else:
            P.add("vector", lambda e, s_t=s_t, ps=ps, bit=bit, lo=lo: e.tensor_tensor(
                out=s_t[:], in0=ps[:], in1=bit[:, lo: lo + 512], op=ALU.add),
                reads=[b_ps, b_bi], writes=[b_s])
        p_t, b_p = pT.next()
        P.add("scalar", lambda e, p_t=p_t, s_t=s_t: e.activation(out=p_t[:], in_=s_t[:], func=AF.Exp),
              reads=[b_s], writes=[b_p])
        if j == 0:
            acc = psO.next() + psD.next()
        po, b_po, pd, b_pd = acc
        kt = (TB * blk + 128 * j) // 128
        P.add("tensor", lambda e, po=po, vt=vt, kt=kt, p_t=p_t, j=j: e.matmul(
            po[:], lhsT=vt[:, kt, :], rhs=p_t[:], start=(j == 0), stop=(j == 7)),
            reads=[b_v, b_p], writes=[b_po])
        P.add("tensor", lambda e, pd=pd, p_t=p_t, j=j: e.matmul(
            pd[:], lhsT=ones[:], rhs=p_t[:], start=(j == 0), stop=(j == 7)),
            reads=[b_ones, b_p], writes=[b_pd])
        if j == 7:
            r_t, b_r = rden.next()
            o_t, b_o = ost.next()
            P.add("vector", lambda e, r_t=r_t, pd=pd: e.reciprocal(out=r_t[:], in_=pd[:]),
                  reads=[b_pd], writes=[b_r])
            P.add("vector", lambda e, o_t=o_t, po=po, r_t=r_t: e.tensor_tensor(
                out=o_t[:], in0=po[:], in1=r_t[:], op=ALU.mult), reads=[b_po, b_r], writes=[b_o])
            P.dma("gpsimd", oT0[h * 128:(h + 1) * 128, blk * TB:(blk + 1) * TB], o_t[:], reads=[b_o])
    P.emit()


def phase_ffn(nc, oT, nhc, wo_b, x_res, res_row0, g_ffn, w1_b, w2_b, out, final_gain=None,
              nblk=NOWN // TB):
    P = Prog(nc)
    nrm = Norm(P, "n", [g_ffn])
    xres = P.sbuf("xres", [128, 4, D], F32)
    b_x = [Buf("x%d" % i) for i in range(4)]
    oTb = SbufRot(P, "oTb", 1, [128, nhc, TB], BF16)
    xnT = P.sbuf("xnT", [128, DC, TB], BF16)
    b_xnT = Buf("xnT")
    h1T = P.sbuf("h1T", [128, DFF // 128, TB], BF16)
    b_h1 = [Buf("h1_%d" % i) for i in range(16)]
    ws = WStream(P, "w", 3)
    psA = PsumRot(P, "psA", 4)
    psB = PsumRot(P, "psB", 2)
    rl = SbufRot(P, "rl", 2, [128, TB], F32)
    if final_gain is not None:
        gbc = P.sbuf("gbc", [128, D], F32)
        b_gbc = Buf()
        P.dma("sync", gbc[:], final_gain.partition_broadcast(128), writes=[b_gbc])
        fin = SbufRot(P, "fin", 1, [128, D], F32)
    for b in range(nblk):
        for tt in range(4):
            r = res_row0 + b * TB + tt * 128
            P.dma("sync", xres[:, tt, :], x_res[r:r + 128, :], writes=[b_x[tt]])
        oT_t, b_oT = oTb.next()
        P.dma("sync", oT_t[:], oT[:, b * TB:(b + 1) * TB].rearrange("(c p) t -> p c t", p=128), writes=[b_oT])
        for fb in range(4):
            wt, b_wt = ws.load(wview(wo_b, 0, nhc * 128, fb * 512, 512), kc=(None if nhc == DC else nhc))
            for tt in range(4):
                ps, b_ps = psA.next()
                mm_as(P, ps, b_ps, wt, b_wt, oT_t, b_oT, tt, kc=nhc)
                P.add("vector", lambda e, ps=ps, tt=tt, fb=fb: e.tensor_tensor(
                    out=xres[:, tt, fb * 512:(fb + 1) * 512], in0=ps[:], in1=xres[:, tt, fb * 512:(fb + 1) * 512],
                    op=ALU.add), reads=[b_ps, b_x[tt]], writes=[b_x[tt]])
        for tt in range(4):
            nrm.run(xres[:, tt, :], b_x[tt], xnT[:, :, tt * 128:(tt + 1) * 128], b_xnT)
        for fg in range(16):
            wt, b_wt = ws.load(wview(w1_b, 0, D, fg * 512, 512))
            for fl in range(4):
                ps, b_ps = psB.next()
                mm_ws(P, ps, b_ps, wt, b_wt, fl, xnT, b_xnT)
                r_t, b_r = rl.next()
                P.add("scalar", lambda e, r_t=r_t, ps=ps: e.activation(out=r_t[:], in_=ps[:], func=AF.Relu),
                      reads=[b_ps], writes=[b_r])
                P.add("gpsimd", lambda e, r_t=r_t, fc=fg * 4 + fl: e.tensor_tensor(
                    out=h1T[:, fc, :], in0=r_t[:], in1=r_t[:], op=ALU.mult), reads=[b_r], writes=[b_h1[fg]])
        for db in range(4):
            accs = [psA.next() for _ in range(4)]
            for fgg in range(4):
                wt, b_wt = ws.load(wview(w2_b, fgg * 2048, 2048, db * 512, 512))
                for tt in range(4):
                    mm_as(P, accs[tt][0], accs[tt][1], wt, b_wt, h1T, b_h1[4 * fgg:4 * fgg + 4], tt, kc=16,
                          first=(fgg == 0), last=(fgg == 3), c0=fgg * 16)
            for tt in range(4):
                ps, b_ps = accs[tt]
                P.add("vector", lambda e, ps=ps, tt=tt, db=db: e.tensor_tensor(
                    out=xres[:, tt, db * 512:(db + 1) * 512], in0=ps[:], in1=xres[:, tt, db * 512:(db + 1) * 512],
                    op=ALU.add), reads=[b_ps, b_x[tt]], writes=[b_x[tt]])
        for tt in range(4):
            r = b * TB + tt * 128
            if final_gain is None:
                P.dma("gpsimd", out[r:r + 128, :], xres[:, tt, :], reads=[b_x[tt]])
            else:
                s = nrm.stats(xres[:, tt, :], b_x[tt])
                f_t, b_f = fin.next()
                P.add("scalar", lambda e, f_t=f_t, tt=tt, s=s: e.activation(
                    out=f_t[:], in_=xres[:, tt, :], func=AF.Copy, scale=nrm.rstd[s][:]),
                    reads=[b_x[tt], nrm.b_rstd[s]], writes=[b_f])
                P.add("gpsimd", lambda e, f_t=f_t: e.tensor_tensor(out=f_t[:], in0=f_t[:], in1=gbc[:], op=ALU.mult),
                      reads=[b_f, b_gbc], writes=[b_f])
                P.dma("gpsimd", out[r:r + 128, :], f_t[:], reads=[b_f])
    P.emit()


def phase_qkv1(nc, x_ext, g_dil, wd_b, rtab, qT1, kT1, v1, nblk=E1 // TB):
    P = Prog(nc)
    nrm = Norm(P, "n1", [g_dil])
    xrot = SbufRot(P, "xin", 2, [128, D], F32)
    xnT = SbufRot(P, "xnT", 2, [128, DC, TB], BF16)
    ws = WStream(P, "w", 3)
    psA = PsumRot(P, "psA", 4)
    psT = PsumRot(P, "psT", 2, (128, 4, 128), BF16)
    rt = SbufRot(P, "rt", 2, [128, 4, 4, 128], F32)
    qtok = SbufRot(P, "qtok", 2, [128, 4, 128], BF16)
    tcs = SbufRot(P, "tcs", 2, [128, 2, 128], F32)
    xrr = SbufRot(P, "xrr", 2, [128, 128], F32)
    stT = [P.sbuf("stT%d" % i, [128, 4, TB], BF16) for i in range(2)]
    b_stT = [[Buf() for _ in range(4)] for _ in range(2)]
    stv = SbufRot(P, "stv", 3, [128, 512], BF16)
    n_ev = 0
    n_st = 0
    ident, b_ident = nrm.ident, nrm.b_ident
    for b in range(nblk):
        xT, b_xT = xnT.next()
        load_norm_block(P, nrm, xrot, x_ext, b * TB, xT, b_xT)
        r_t, b_rt = rt.next()
        P.dma("sync", r_t[:], rtab[b * TB:(b + 1) * TB].rearrange("(t p) k i -> p t k i", p=128), writes=[b_rt])
        for fb in range(18):
            g, rem = divmod(fb, 6)
            jq, hh = divmod(rem, 2)
            wt, b_wt = ws.load(wview(wd_b, 0, D, fb * 512, 512))
            if jq < 2:
                k = n_st % 2
                n_st += 1
                st, bst = stT[k], b_stT[k]
            for tt in range(4):
                ps, b_ps = psA.next()
                mm_as(P, ps, b_ps, wt, b_wt, xT, b_xT, tt)
                if jq == 2:
                    sv, b_sv = stv.next()
                    eng = "scalar" if n_ev % 2 == 0 else "vector"
                    n_ev += 1
                    evac(P, eng, sv[:], ps[:], [b_ps], [b_sv])
                    r0 = b * TB + tt * 128
                    c0 = g * 1024 + hh * 512
                    P.dma("gpsimd", v1[r0:r0 + 128, c0:c0 + 512], sv[:], reads=[b_sv])
                    continue
                psv = ps[:].rearrange("p (h d) -> p h d", h=4)
                q_t, b_q = qtok.next()
                t_t, b_t = tcs.next()
                evac(P, "scalar", q_t[:, :, 32:128], psv[:, :, 32:128], [b_ps], [b_q],
                     scale=(QSCALE if jq == 0 else None))
                cs = r_t[:, tt, 2 * jq, :].rearrange("p (h i) -> p h i", h=4)
                sn = r_t[:, tt, 2 * jq + 1, :].rearrange("p (h i) -> p h i", h=4)
                xr, b_xr = xrr.next()
                evac(P, "scalar", xr[:].rearrange("p (h i) -> p h i", h=4), psv[:, :, 0:32], [b_ps], [b_xr])
                cs = r_t[:, tt, 2 * jq, :]
                sn = r_t[:, tt, 2 * jq + 1, :]
                P.add("vector", lambda e, t_t=t_t, xr=xr, cs=cs: e.tensor_tensor(
                    out=t_t[:, 0, :], in0=xr[:], in1=cs, op=ALU.mult), reads=[b_xr, b_rt], writes=[b_t])
                P.add("vector", lambda e, t_t=t_t, xr=xr, sn=sn: e.tensor_tensor(
                    out=t_t[:, 1, :], in0=xr[:], in1=sn, op=ALU.mult), reads=[b_xr, b_rt], writes=[b_t])
                tc3 = t_t[:, 0, :].rearrange("p (h i) -> p h i", h=4)
                ts3 = t_t[:, 1, :].rearrange("p (h i) -> p h i", h=4)
                reng = DBG.get("roteng", "gpsimd")
                P.add(reng, lambda e, tc3=tc3, ts3=ts3, q_t=q_t: e.tensor_tensor(
                    out=q_t[:, :, 0:16], in0=tc3[:, :, 0:16], in1=ts3[:, :, 16:32], op=ALU.subtract),
                    reads=[b_t], writes=[b_q])
                P.add(reng, lambda e, tc3=tc3, ts3=ts3, q_t=q_t: e.tensor_tensor(
                    out=q_t[:, :, 16:32], in0=tc3[:, :, 16:32], in1=ts3[:, :, 0:16], op=ALU.add),
                    reads=[b_t], writes=[b_q])
                p_t, b_p = psT.next()
                for hl in range(4):
                    P.add("tensor", lambda e, p_t=p_t, q_t=q_t, hl=hl: e.transpose(
                        out=p_t[:, hl, :], in_=q_t[:, hl, :], identity=ident[:]),
                        reads=[b_q, b_ident], writes=[b_p])
                eng = "scalar" if n_ev % 2 == 0 else "vector"
                n_ev += 1
                evac(P, eng, st[:, :, tt * 128:(tt + 1) * 128], p_t[:], [b_p], [bst[tt]])
            if jq < 2:
                dst = qT1 if jq == 0 else kT1
                r0 = g * 1024 + hh * 512
                P.dma("gpsimd", dst[r0:r0 + 512, b * TB:(b + 1) * TB].rearrange("(c p) t -> p c t", p=128),
                      st[:], reads=bst)
    P.emit()


DILS = (1, 4, 16)
KB_OFF = (0, 33, 33 + 36)
KB_COLS = 33 + 36 + 48


def phase_dil(nc, qT1, kT1, v1, kb, band, oT1, nheads=8):
    P = Prog(nc)
    ones = P.sbuf("ones", [128, 128], BF16)
    zeros = P.sbuf("zeros", [128, 512], BF16)
    b_c = Buf("consts")
    P.add("gpsimd", lambda e: e.memset(ones[:], 1.0), writes=[b_c])
    P.add("gpsimd", lambda e: e.memset(zeros[:], 0.0), writes=[b_c])
    bandt = P.sbuf("band", [128, 256], BF16)
    kbt = P.sbuf("kb", [128, KB_COLS], F32)
    P.dma("sync", bandt[:], band, writes=[b_c])
    P.dma("sync", kbt[:], kb, writes=[b_c])
    raw = SbufRot(P, "raw", 2, [128, 2, E1], BF16)
    dei = SbufRot(P, "dei", 2, [128, 2, E1], BF16)
    vts = SbufRot(P, "vt", 2, [128, 48, 128], BF16)
    psS = PsumRot(P, "psS", 3)
    psO = PsumRot(P, "psO", 2)
    psD = PsumRot(P, "psD", 2)
    et = SbufRot(P, "et", 3, [128, 256], BF16)
    pT = SbufRot(P, "pT", 3, [128, 256], BF16)
    acc_n = P.sbuf("acc_n", [128, NOWN], F32)
    acc_d = P.sbuf("acc_d", [128, NOWN], F32)
    b_an, b_ad = Buf("acc_n"), Buf("acc_d")
    ost = P.sbuf("ost", [128, NOWN], BF16)
    b_ost = Buf("ost")
    units = [(h, g) for h in range(nheads) for g in range(3)]
    res = {}

    def get_unit(u):
        if u >= len(units) or u in res:
            return res.get(u)
        h, g = units[u]
        dil = DILS[g]
        row0 = g * 1024 + h * 128
        rw, b_rw = raw.next()
        P.dma("sync", rw[:, 0, :], kT1[row0:row0 + 128, :], writes=[b_rw])
        P.dma("sync", rw[:, 1, :], qT1[row0:row0 + 128, :], writes=[b_rw])
        if dil > 1:
            de, b_de = dei.next()
            for s in range(2):
                P.add("gpsimd", lambda e, s=s, de=de, rw=rw, dil=dil: e.tensor_copy(
                    out=de[:, s, :].rearrange("p (r m) -> p r m", r=dil),
                    in_=rw[:, s, :].rearrange("p (m r) -> p r m", r=dil)), reads=[b_rw], writes=[b_de])
        else:
            de, b_de = rw, b_rw
        vt, b_vt = vts.next()
        L = NOWN // dil
        a = H1 // dil
        nt = L // 128 + 1
        vsrc = v1[:, row0:row0 + 128].rearrange("(m r) d -> r m d", r=dil)
        for r in range(dil):
            P.dma("sync", vt[:, r * nt:(r + 1) * nt, :],
                  vsrc[r, a - 64: a - 64 + nt * 128, :].rearrange("(i p) d -> p i d", p=128), writes=[b_vt])
        res[u] = (de, b_de, vt, b_vt)
        return res[u]

    steps = []
    for u, (h, g) in enumerate(units):
        dil = DILS[g]
        L = NOWN // dil
        for r in range(dil):
            for q0 in range(0, L, 512):
                NQ = min(512, L - q0)
                for j in range(NQ // 128 + 1):
                    steps.append((u, r, q0, NQ, j))

    def geom(st):
        u, r, q0, NQ, j = st
        h, g = units[u]
        dil = DILS[g]
        a = H1 // dil
        seq0 = r * (E1 // dil)
        qs = max(0, 128 * j - 128)
        qe = min(NQ, 128 * j + 128)
        moff = qs - (128 * j - 128)
        koff = seq0 + a + q0 - 64 + 128 * j
        qoff = seq0 + a + q0 + qs
        nt = (NOWN // dil) // 128 + 1
        kt = r * nt + q0 // 128 + j
        kbcol = KB_OFF[g] + kt
        return h, g, dil, qs, qe, moff, koff, qoff, kt, kbcol

    def issue_S(st):
        u = st[0]
        de, b_de = get_unit(u)[:2]
        h, g, dil, qs, qe, moff, koff, qoff, kt, kbcol = geom(st)
        n = qe - qs
        ps, b_ps = psS.next()
        P.add("tensor", lambda e: e.matmul(ps[:, :n], lhsT=de[:, 0, koff:koff + 128], rhs=de[:, 1, qoff:qoff + n],
                                           start=True, stop=True), reads=[b_de], writes=[b_ps])
        return ps, b_ps

    get_unit(0)
    nxt = issue_S(steps[0])
    acc = None
    for i, st in enumerate(steps):
        u, r, q0, NQ, j = st
        ps, b_ps = nxt
        if i + 1 < len(steps):
            nxt = issue_S(steps[i + 1])
        if r == 0 and q0 == 0 and j == 0:
            get_unit(u + 1)
        de, b_de, vt, b_vt = get_unit(u)
        h, g, dil, qs, qe, moff, koff, qoff, kt, kbcol = geom(st)
        n = qe - qs
        nkt = NQ // 128 + 1
        if j == 0:
            acc = psO.next() + psD.next()
            po, b_po, pd, b_pd = acc
            P.add("tensor", lambda e, po=po, NQ=NQ: e.matmul(po[:, :NQ], lhsT=zeros[:, 0:128], rhs=zeros[:, :NQ],
                                                          start=True, stop=False), reads=[b_c], writes=[b_po])
            P.add("tensor", lambda e, pd=pd, NQ=NQ: e.matmul(pd[:, :NQ], lhsT=zeros[:, 0:128], rhs=zeros[:, :NQ],
                                                          start=True, stop=False), reads=[b_c], writes=[b_pd])
        po, b_po, pd, b_pd = acc
        e_t, b_e = et.next()
        P.add("scalar", lambda e, e_t=e_t, ps=ps, n=n, kbcol=kbcol: e.activation(
            out=e_t[:, :n], in_=ps[:, :n], func=AF.Exp, bias=kbt[:, kbcol:kbcol + 1]),
            reads=[b_ps, b_c], writes=[b_e])
        p_t, b_p = pT.next()
        P.add("gpsimd", lambda e, p_t=p_t, e_t=e_t, n=n, moff=moff: e.tensor_tensor(
            out=p_t[:, :n], in0=e_t[:, :n], in1=bandt[:, moff:moff + n], op=ALU.mult),
            reads=[b_e, b_c], writes=[b_p])
        last = (j == nkt - 1)
        P.add("tensor", lambda e, po=po, vt=vt, kt=kt, p_t=p_t, qs=qs, qe=qe, n=n, last=last: e.matmul(
            po[:, qs:qe], lhsT=vt[:, kt, :], rhs=p_t[:, :n], start=False, stop=last),
            reads=[b_vt, b_p], writes=[b_po])
        P.add("tensor", lambda e, pd=pd, p_t=p_t, qs=qs, qe=qe, n=n, last=last: e.matmul(
            pd[:, qs:qe], lhsT=ones[:], rhs=p_t[:, :n], start=False, stop=last),
            reads=[b_c, b_p], writes=[b_pd])
        if last:
            an = acc_n[:].rearrange("p (m r) -> p m r", r=dil)[:, q0:q0 + NQ, r]
            ad = acc_d[:].rearrange("p (m r) -> p m r", r=dil)[:, q0:q0 + NQ, r]
            if g == 0:
                P.add("scalar", lambda e, an=an, po=po, NQ=NQ: e.copy(out=an, in_=po[:, :NQ]),
                      reads=[b_po, b_ost], writes=[b_an])
                P.add("vector", lambda e, ad=ad, pd=pd, NQ=NQ: e.tensor_copy(out=ad, in_=pd[:, :NQ]),
                      reads=[b_pd, b_ost], writes=[b_ad])
            else:
                P.add("vector", lambda e, an=an, po=po, NQ=NQ: e.tensor_tensor(
                    out=an, in0=po[:, :NQ], in1=an, op=ALU.add), reads=[b_po, b_an], writes=[b_an])
                P.add("vector", lambda e, ad=ad, pd=pd, NQ=NQ: e.tensor_tensor(
                    out=ad, in0=pd[:, :NQ], in1=ad, op=ALU.add), reads=[b_pd, b_ad], writes=[b_ad])
            end_of_head = (g == 2 and r == dil - 1 and q0 + NQ >= NOWN // dil)
            if end_of_head:
                P.add("vector", lambda e: e.reciprocal(out=acc_d[:], in_=acc_d[:]), reads=[b_ad], writes=[b_ad])
                P.add("gpsimd", lambda e: e.tensor_tensor(out=ost[:], in0=acc_n[:], in1=acc_d[:], op=ALU.mult),
                      reads=[b_an, b_ad], writes=[b_ost])
                P.dma("gpsimd", oT1[h * 128:(h + 1) * 128, :], ost[:], reads=[b_ost])
    P.emit()


def _in(nc, name, shape, dtype=F32):
    return nc.dram_tensor(name, list(shape), dtype, kind="ExternalInput").ap()


DBG_OUT = set()
DBG = {}


def _scr(nc, name, shape, dtype=BF16):
    if name in DBG_OUT:
        return nc.dram_tensor(name, list(shape), dtype, kind="ExternalOutput").ap()
    return nc.dram_tensor(name, list(shape), dtype).ap()


def build_l0():
    nc = bass.Bass("TRN2", target_bir_lowering=False)
    x_ext = _in(nc, "x_ext", [E0, D])
    g_na = _in(nc, "na_norm", [D])
    g_f0 = _in(nc, "ffn0_norm", [D])
    wqkv = _in(nc, "na_wqkv", [D, 3 * D])
    wo = _in(nc, "na_wo", [D, D])
    w1 = _in(nc, "ffn0_w1", [D, DFF])
    w2 = _in(nc, "ffn0_w2", [DFF, D])
    bint = _in(nc, "bint", [16, 128, 22 * 64])
    bfirst = _in(nc, "bfirst", [16, 128, 8 * 256])
    blast = _in(nc, "blast", [16, 128, 8 * 256])
    x2 = nc.dram_tensor("x2", [NOWN, D], F32, kind="ExternalOutput").ap()
    wqkv_b = _scr(nc, "wqkv_b", [D, 3 * D])
    wo_b = _scr(nc, "wo_b", [D, D])
    w1_b = _scr(nc, "w1_b", [D, DFF])
    w2_b = _scr(nc, "w2_b", [DFF, D])
    qT0 = _scr(nc, "qT0", [D, E0])
    kT0 = _scr(nc, "kT0", [D, E0])
    v0 = _scr(nc, "v0", [E0, D])
    oT0 = _scr(nc, "oT0", [D, NOWN])
    cast_weights(nc, [(wqkv_b, wqkv), (wo_b, wo), (w1_b, w1), (w2_b, w2)])
    phase_qkv0(nc, x_ext, g_na, wqkv_b, qT0, kT0, v0)
    phase_na(nc, qT0, kT0, v0, bint, bfirst, blast, oT0)
    phase_ffn(nc, oT0, 16, wo_b, x_ext, H0, g_f0, w1_b, w2_b, x2)
    return nc


def build_l1(upto=3):
    nc = bass.Bass("TRN2", target_bir_lowering=False)
    x_ext = _in(nc, "x2_ext", [E1, D])
    g_d = _in(nc, "dil_norm", [D])
    g_f1 = _in(nc, "ffn1_norm", [D])
    g_fin = _in(nc, "final_norm", [D])
    wd = _in(nc, "dil_wqkv", [D, 9 * 1024])
    wo = _in(nc, "dil_wo", [1024, D])
    w1 = _in(nc, "ffn1_w1", [D, DFF])
    w2 = _in(nc, "ffn1_w2", [DFF, D])
    rtab = _in(nc, "rtab", [E1, 4, 128])
    kb = _in(nc, "kb", [128, KB_COLS])
    band = _in(nc, "band", [128, 256], BF16)
    out = nc.dram_tensor("out", [NOWN, D], F32, kind="ExternalOutput").ap()
    wd_b = _scr(nc, "wd_b", [D, 9 * 1024])
    wo_b = _scr(nc, "wo1_b", [1024, D])
    w1_b = _scr(nc, "w11_b", [D, DFF])
    w2_b = _scr(nc, "w21_b", [DFF, D])
    qT1 = _scr(nc, "qT1", [3 * 1024, E1])
    kT1 = _scr(nc, "kT1", [3 * 1024, E1])
    v1 = _scr(nc, "v1", [E1, 3 * 1024])
    oT1 = _scr(nc, "oT1", [1024, NOWN])
    cast_weights(nc, [(wd_b, wd), (wo_b, wo), (w1_b, w1), (w2_b, w2)])
    phase_qkv1(nc, x_ext, g_d, wd_b, rtab, qT1, kT1, v1)
    if upto >= 2:
        phase_dil(nc, qT1, kT1, v1, kb, band, oT1)
    if upto >= 3:
        phase_ffn(nc, oT1, 8, wo_b, x_ext, H1, g_f1, w1_b, w2_b, out, final_gain=g_fin)
    return nc


def na_tables(rpb, core):
    R0 = 64 * (core % 4)
    kcol = np.arange(64)
    qcol = np.arange(64)
    cstart = np.clip(qcol - 8, 0, 48)
    cmask = (kcol[:, None] >= cstart[None, :]) & (kcol[:, None] < cstart[None, :] + 16)
    cidx = np.clip(kcol[:, None] - qcol[None, :] + 15, 0, 30)
    par = np.arange(2)

    def table(off, valid):
        oi = np.clip(off + 7, 0, 14)
        lead = off.shape[:-1]
        vals = rpb[:, oi][..., cidx]
        ok = valid[..., None, None] & cmask
        vals = np.where(ok[None], vals, np.float32(NEGB))
        nl = len(lead)
        perm = (0, 1 + nl, 2 + nl) + tuple(range(1, 1 + nl)) + (3 + nl,)
        return np.ascontiguousarray(vals.transpose(perm)).reshape((16, 128) + lead + (64,))

    dd = np.arange(22)
    off = 10 - dd[:, None] + par[None, :]
    bint = table(off, (off >= -4) & (off <= 3)).reshape(16, 128, 22 * 64)

    def edge(blk, qrs):
        j = np.arange(8)
        R = R0 + 8 * blk + qrs
        Kr = R0 + 8 * blk - 4 + 2 * j[:, None, None] + par[None, None, :]
        rs = np.clip(R - 4, 0, 248)[None, :, None]
        off = Kr - R[None, :, None]
        valid = (Kr >= rs) & (Kr < rs + 8)
        return table(off, np.broadcast_to(valid, off.shape)).reshape(16, 128, 8 * 256)

    bfirst = edge(0, np.arange(0, 4))
    blast = edge(7, np.arange(4, 8))
    return bint.astype(np.float32), bfirst.astype(np.float32), blast.astype(np.float32)


def rot_table(core):
    tok0 = (core % 4) * NOWN - H1
    pos = (tok0 + np.arange(E1)).astype(np.float32)
    inv = (np.float32(500000.0) ** (-np.arange(0, 32, 2, dtype=np.float32) / np.float32(32))).astype(np.float32)
    ang = (pos[:, None] * inv[None, :]).astype(np.float32).astype(np.float64)
    c = np.cos(ang).astype(np.float32)
    s = np.sin(ang).astype(np.float32)
    c2 = np.concatenate([c, c], -1)
    s2 = np.concatenate([s, s], -1)
    qs = np.float32(QSCALE)
    t = np.stack([c2 * qs, s2 * qs, c2, s2], axis=1)
    return np.ascontiguousarray(np.tile(t, (1, 1, 4))).astype(np.float32)


def key_bias(core):
    tok0 = (core % 4) * NOWN - H1
    kb = np.zeros((128, KB_COLS), np.float32)
    p = np.arange(128)
    for g, dil in enumerate(DILS):
        a = H1 // dil
        nt = (NOWN // dil) // 128 + 1
        for r in range(dil):
            for i in range(nt):
                e = r + dil * (a - 64 + 128 * i + p)
                glob = tok0 + e
                kb[:, KB_OFF[g] + r * nt + i] = np.where((glob >= 0) & (glob < SEQ), 0.0, NEGB)
    return kb


def band_mask():
    p = np.arange(128)[:, None]
    q = np.arange(256)[None, :]
    return ((p >= q - 128) & (p <= q)).astype(np.float32).astype(ml_dtypes.bfloat16)


def ext_rows(xb, t0, halo):
    n = NOWN + 2 * halo
    out = np.zeros((n, xb.shape[1]), xb.dtype)
    lo, hi = t0 - halo, t0 + NOWN + halo
    a, b = max(lo, 0), min(hi, xb.shape[0])
    out[a - lo:b - lo] = xb[a:b]
    return out


_CACHE = {}


def _prog(name, fn):
    if name not in _CACHE:
        _CACHE[name] = fn()
    return _CACHE[name]


def kernel(x, na_norm, na_wqkv, na_rpb, na_wo, ffn0_norm, ffn0_w1, ffn0_w2,
           dil_norm, dil_wqkv, dil_wo, ffn1_norm, ffn1_w1, ffn1_w2, final_norm):
    f32 = lambda a: np.ascontiguousarray(np.asarray(a, dtype=np.float32))
    x = f32(x)
    cores = list(range(NCORES))
    nc0 = _prog("l0", build_l0)
    nc1 = _prog("l1", build_l1)
    common0 = {"na_norm": f32(na_norm), "ffn0_norm": f32(ffn0_norm), "na_wqkv": f32(na_wqkv),
               "na_wo": f32(na_wo), "ffn0_w1": f32(ffn0_w1), "ffn0_w2": f32(ffn0_w2)}
    rpb = f32(na_rpb)
    maps = []
    for c in cores:
        m = dict(common0)
        m["x_ext"] = ext_rows(x[c // 4], (c % 4) * NOWN, H0)
        m["bint"], m["bfirst"], m["blast"] = na_tables(rpb, c)
        maps.append(m)
    r0 = run_bass_kernel_spmd(nc0, maps, core_ids=cores)
    x2 = np.stack([np.concatenate([r0.results[b * 4 + i]["x2"] for i in range(4)], 0) for b in range(2)], 0)
    nc1 = _prog("l1", build_l1)
    common1 = {"dil_norm": f32(dil_norm), "ffn1_norm": f32(ffn1_norm), "final_norm": f32(final_norm),
               "dil_wqkv": f32(dil_wqkv), "dil_wo": f32(dil_wo), "ffn1_w1": f32(ffn1_w1),
               "ffn1_w2": f32(ffn1_w2), "band": band_mask()}
    maps = []
    for c in cores:
        m = dict(common1)
        m["x2_ext"] = ext_rows(x2[c // 4], (c % 4) * NOWN, H1)
        m["rtab"] = rot_table(c)
        m["kb"] = key_bias(c)
        maps.append(m)
    r1 = run_bass_kernel_spmd(nc1, maps, core_ids=cores)
    out = np.stack([np.concatenate([r1.results[b * 4 + i]["out"] for i in range(4)], 0) for b in range(2)], 0)
    return out.astype(np.float32)
```
